# Optimizing a Trainium2 kernel written in Bass

```python
import math
import jax, jax.numpy as jnp
from jax import lax
import numpy as np


D_MODEL = 1024
BATCH = 2
SEQ = 8192
DEPTH = 2
DEC_BATCH = 128
DEC_SEQ = 8
PAST_LEN = 8192
PAGE_SIZE = 128

N_A_LAYERS = DEPTH // 2
N_B_LAYERS = DEPTH - N_A_LAYERS
GLA_HEADS = 4
GLA_DK = D_MODEL // 2 // GLA_HEADS
GLA_DV = D_MODEL // GLA_HEADS
GLA_RANK = 16
GLA_TAU = 16.0
GLA_CHUNK = 64
IN_A_COLS = 2 * GLA_HEADS * GLA_DK + 2 * GLA_HEADS * GLA_DV + GLA_RANK
SWA_HEADS = 16
SWA_KV_HEADS = 4
SWA_GROUP = SWA_HEADS // SWA_KV_HEADS
SWA_HEAD_DIM = 64
WINDOW = 128
REL_BUCKETS = 32
REL_MAX_DIST = 128
D_FF = 4 * D_MODEL
PLE_DIM = 256
EPS = 1e-6

kernel_name = 'yoco_gla_swa_sink_decoder_step'


def rmsnorm(x, gain):
    xf = x.astype(jnp.float32)
    xf = xf * lax.rsqrt(jnp.mean(xf * xf, axis=-1, keepdims=True) + EPS)
    return (xf * gain.astype(jnp.float32)).astype(x.dtype)


def rel_bucket(d):
    max_exact = REL_BUCKETS // 2
    df = jnp.maximum(d, 1).astype(jnp.float32)
    large = max_exact + (jnp.log(df / max_exact) / math.log(REL_MAX_DIST / max_exact)
                         * (REL_BUCKETS - max_exact)).astype(jnp.int32)
    large = jnp.minimum(large, REL_BUCKETS - 1)
    return jnp.where(d < max_exact, d, large)


def gla_recurrence(q, k, v, lg, s0):
    b, s = q.shape[:2]
    c = GLA_CHUNK if s % GLA_CHUNK == 0 else s
    nc = s // c

    def to_chunks(t):
        return t.reshape(b, nc, c, GLA_HEADS, t.shape[-1]).transpose(1, 0, 3, 2, 4)

    causal = jnp.tril(jnp.ones((c, c), bool))[:, :, None]

    def step(state, inp):
        qc, kc, vc, lc = [t.astype(jnp.float32) for t in inp]
        bc = jnp.cumsum(lc, axis=-2)
        diff = bc[:, :, :, None, :] - bc[:, :, None, :, :]
        decay = jnp.where(causal, jnp.exp(jnp.where(causal, diff, 0.0)), 0.0)
        attn = jnp.einsum('bhid,bhjd,bhijd->bhij', qc, kc, decay)
        o = jnp.einsum('bhij,bhje->bhie', attn, vc) + jnp.einsum('bhid,bhde->bhie', qc * jnp.exp(bc), state)
        last = bc[:, :, -1:, :]
        new_state = (jnp.exp(last[:, :, 0, :])[..., None] * state
                     + jnp.einsum('bhjd,bhje->bhde', kc * jnp.exp(last - bc), vc))
        return new_state, o

    s_fin, o = lax.scan(step, s0.astype(jnp.float32),
                        (to_chunks(q), to_chunks(k), to_chunks(v), to_chunks(lg)))
    o = o.transpose(1, 0, 3, 2, 4).reshape(b, s, GLA_HEADS, GLA_DV)
    return o, s_fin


def gla_mixer(hn, s0, w_in, w_a2, b_a2, o_gain, w_out):
    b, s, _ = hn.shape
    nq = GLA_HEADS * GLA_DK
    nv = GLA_HEADS * GLA_DV
    proj = hn @ w_in
    q, k, v, g, a = jnp.split(proj, [nq, 2 * nq, 2 * nq + nv, 2 * nq + 2 * nv], axis=-1)
    lg = jax.nn.log_sigmoid((a @ w_a2 + b_a2).astype(jnp.float32)) / GLA_TAU
    o, s_fin = gla_recurrence(q.reshape(b, s, GLA_HEADS, GLA_DK) * GLA_DK ** -0.5,
                              k.reshape(b, s, GLA_HEADS, GLA_DK),
                              v.reshape(b, s, GLA_HEADS, GLA_DV),
                              lg.reshape(b, s, GLA_HEADS, GLA_DK), s0)
    o = rmsnorm(o.astype(hn.dtype), o_gain).reshape(b, s, nv)
    return (o * jax.nn.silu(g)) @ w_out, s_fin.astype(hn.dtype)


def swa_core(q, k, v, qpos, kpos, rel_bias, sinks):
    n, nq = qpos.shape
    nk = kpos.shape[1]
    d = qpos[:, :, None] - kpos[:, None, :]
    valid = (d >= 0) & (d < WINDOW) & (kpos[:, None, :] >= 0)
    bias = rel_bias.astype(jnp.float32).T[:, rel_bucket(jnp.maximum(d, 0))]
    bias = bias.reshape(SWA_KV_HEADS, SWA_GROUP, n, nq, nk).transpose(2, 0, 1, 3, 4)
    logits = jnp.einsum('bnqhgd,bnshd->bnhgqs', q, k).astype(jnp.float32) * SWA_HEAD_DIM ** -0.5 + bias
    logits = jnp.where(valid[None, :, None, None], logits, -jnp.inf)
    sink = sinks.astype(jnp.float32).reshape(SWA_KV_HEADS, SWA_GROUP)[:, :, None, None]
    m = jnp.maximum(jnp.max(logits, axis=-1, keepdims=True), sink)
    e = jnp.exp(logits - m)
    p = e / (jnp.sum(e, axis=-1, keepdims=True) + jnp.exp(sink - m))
    return jnp.einsum('bnhgqs,bnshd->bnqhgd', p.astype(v.dtype), v)


def swa_mixer(hn, k_new, v_new, past_k, past_v, w_q, w_o, sinks, rel_bias):
    b, s, _ = hn.shape
    q = (hn @ w_q).reshape(b, s, SWA_KV_HEADS, SWA_GROUP, SWA_HEAD_DIM)
    if past_k is None:
        nb = s // WINDOW
        q = q.reshape(b, nb, WINDOW, SWA_KV_HEADS, SWA_GROUP, SWA_HEAD_DIM)

        def band(t):
            tb = jnp.concatenate([jnp.zeros_like(t[:, :WINDOW]), t], axis=1)
            tb = tb.reshape(b, nb + 1, WINDOW, SWA_KV_HEADS, SWA_HEAD_DIM)
            return jnp.concatenate([tb[:, :-1], tb[:, 1:]], axis=2)

        kb, vb = band(k_new), band(v_new)
        qpos = jnp.arange(s, dtype=jnp.int32).reshape(nb, WINDOW)
        kpos = ((jnp.arange(nb, dtype=jnp.int32)[:, None] - 1) * WINDOW
                + jnp.arange(2 * WINDOW, dtype=jnp.int32)[None, :])
    else:
        n_past = past_k.shape[1]
        kb = jnp.concatenate([past_k, k_new], axis=1)[:, None]
        vb = jnp.concatenate([past_v, v_new], axis=1)[:, None]
        q = q[:, None]
        qpos = (PAST_LEN + jnp.arange(s, dtype=jnp.int32))[None]
        kpos = (PAST_LEN - n_past + jnp.arange(n_past + s, dtype=jnp.int32))[None]
    o = swa_core(q, kb, vb, qpos, kpos, rel_bias, sinks)
    return o.reshape(b, s, SWA_HEADS * SWA_HEAD_DIM) @ w_o


def sq_relu_mlp(hn, w_up, w_down):
    return jnp.square(jax.nn.relu(hn @ w_up)) @ w_down


def trunk(x, p, gla_state, past_k, past_v, w):
    h = x
    b, s, _ = x.shape
    gla_states = []
    k_sh = v_sh = None
    for i in range(DEPTH):
        hn = rmsnorm(h, w['norm_mix'][i])
        if i < N_A_LAYERS:
            y, s_fin = gla_mixer(hn, gla_state[i], w['w_in_a'][i], w['w_a2'][i], w['b_a2'][i],
                                 w['gla_o_gain'][i], w['w_out_a'][i])
            gla_states.append(s_fin)
        else:
            j = i - N_A_LAYERS
            y = swa_mixer(hn, k_sh, v_sh, past_k, past_v, w['w_q_b'][j], w['w_o_b'][j],
                          w['sinks'][j], w['rel_bias'])
        h = h + y
        h = h + sq_relu_mlp(rmsnorm(h, w['norm_mlp'][i]), w['w_up'][i], w['w_down'][i])
        gate = jax.nn.sigmoid(rmsnorm(h, w['norm_ple'][i]) @ w['w_ple_gate'][i])
        h = h + gate * (p[i] @ w['w_ple'][i])
        if i == N_A_LAYERS - 1:
            kv = rmsnorm(h, w['norm_kv']) @ w['w_kv']
            k_sh, v_sh = jnp.split(kv, 2, axis=-1)
            k_sh = k_sh.reshape(b, s, SWA_KV_HEADS, SWA_HEAD_DIM)
            v_sh = v_sh.reshape(b, s, SWA_KV_HEADS, SWA_HEAD_DIM)
    y = rmsnorm(h, w['norm_final'])
    if past_k is None:
        keep = min(WINDOW, s)
        new_k, new_v = k_sh[:, s - keep:], v_sh[:, s - keep:]
    else:
        n_past = past_k.shape[1]
        new_k = jnp.concatenate([past_k, k_sh], axis=1)[:, -n_past:]
        new_v = jnp.concatenate([past_v, v_sh], axis=1)[:, -n_past:]
    return y, jnp.stack(gla_states), new_k, new_v


def setup_inputs(seed: int = 0) -> dict:
    key = jax.random.key(seed)
    ks = iter(jax.random.split(key, 32))

    def nrm(shape, scale):
        return jax.random.normal(next(ks), shape, jnp.float32) * scale

    def gain(shape):
        return 1.0 + nrm(shape, 0.02)

    n_win = min(WINDOW, PAST_LEN)
    return {
        'x_prompt': nrm((BATCH, SEQ, D_MODEL), 1.0),
        'x_sample': nrm((DEC_BATCH, DEC_SEQ, D_MODEL), 1.0),
        'state_gla': nrm((N_A_LAYERS, DEC_BATCH, GLA_HEADS, GLA_DK, GLA_DV), 0.5),
        'cache_win_k': nrm((DEC_BATCH, n_win, SWA_KV_HEADS, SWA_HEAD_DIM), 1.0),
        'cache_win_v': nrm((DEC_BATCH, n_win, SWA_KV_HEADS, SWA_HEAD_DIM), 1.0),
        'p_prompt': nrm((DEPTH, BATCH, SEQ, PLE_DIM), 1.0),
        'p_sample': nrm((DEPTH, DEC_BATCH, DEC_SEQ, PLE_DIM), 1.0),
        'norm_mix': gain((DEPTH, D_MODEL)),
        'norm_mlp': gain((DEPTH, D_MODEL)),
        'norm_ple': gain((DEPTH, D_MODEL)),
        'norm_kv': gain((D_MODEL,)),
        'norm_final': gain((D_MODEL,)),
        'w_in_a': nrm((N_A_LAYERS, D_MODEL, IN_A_COLS), D_MODEL ** -0.5),
        'w_a2': nrm((N_A_LAYERS, GLA_RANK, GLA_HEADS * GLA_DK), GLA_RANK ** -0.5),
        'b_a2': nrm((N_A_LAYERS, GLA_HEADS * GLA_DK), 0.1),
        'gla_o_gain': gain((N_A_LAYERS, GLA_DV)),
        'w_out_a': nrm((N_A_LAYERS, GLA_HEADS * GLA_DV, D_MODEL), (GLA_HEADS * GLA_DV) ** -0.5),
        'w_kv': nrm((D_MODEL, 2 * SWA_KV_HEADS * SWA_HEAD_DIM), D_MODEL ** -0.5),
        'w_q_b': nrm((N_B_LAYERS, D_MODEL, SWA_HEADS * SWA_HEAD_DIM), D_MODEL ** -0.5),
        'w_o_b': nrm((N_B_LAYERS, SWA_HEADS * SWA_HEAD_DIM, D_MODEL), (SWA_HEADS * SWA_HEAD_DIM) ** -0.5),
        'sinks': nrm((N_B_LAYERS, SWA_HEADS), 0.5),
        'rel_bias': nrm((REL_BUCKETS, SWA_HEADS), 0.5),
        'w_up': nrm((DEPTH, D_MODEL, D_FF), D_MODEL ** -0.5),
        'w_down': nrm((DEPTH, D_FF, D_MODEL), D_FF ** -0.5),
        'w_ple': nrm((DEPTH, PLE_DIM, D_MODEL), PLE_DIM ** -0.5),
        'w_ple_gate': nrm((DEPTH, D_MODEL, D_MODEL), D_MODEL ** -0.5),
    }


def reference(x_prompt, x_sample, state_gla, cache_win_k, cache_win_v, p_prompt, p_sample,
              norm_mix, norm_mlp, norm_ple, norm_kv, norm_final,
              w_in_a, w_a2, b_a2, gla_o_gain, w_out_a,
              w_kv, w_q_b, w_o_b, sinks, rel_bias,
              w_up, w_down, w_ple, w_ple_gate):
    w = dict(norm_mix=norm_mix, norm_mlp=norm_mlp, norm_ple=norm_ple, norm_kv=norm_kv,
             norm_final=norm_final, w_in_a=w_in_a, w_a2=w_a2, b_a2=b_a2, gla_o_gain=gla_o_gain,
             w_out_a=w_out_a, w_kv=w_kv, w_q_b=w_q_b, w_o_b=w_o_b, sinks=sinks, rel_bias=rel_bias,
             w_up=w_up, w_down=w_down, w_ple=w_ple, w_ple_gate=w_ple_gate)
    b = x_prompt.shape[0]
    gla0_prompt = jnp.zeros((N_A_LAYERS, b, GLA_HEADS, GLA_DK, GLA_DV), x_prompt.dtype)
    y_prompt, state_gla_prompt, cache_win_k_prompt, cache_win_v_prompt = trunk(
        x_prompt, p_prompt, gla0_prompt, None, None, w)
    y_sample, state_gla_sample, cache_win_k_sample, cache_win_v_sample = trunk(
        x_sample, p_sample, state_gla, cache_win_k, cache_win_v, w)
    return (y_prompt, y_sample, state_gla_prompt, state_gla_sample,
            cache_win_k_prompt, cache_win_v_prompt, cache_win_k_sample, cache_win_v_sample)
```

```python
import numpy as np
from contextlib import ExitStack
import concourse.bass as bass
import concourse.mybir as mybir
from concourse.bass_utils import run_bass_kernel_spmd

F32 = mybir.dt.float32
BF16 = mybir.dt.bfloat16
AF = mybir.ActivationFunctionType
ALU = mybir.AluOpType
EPS = 1e-6
NPREB = 47
NSUBP = 17
NSUB = 18
T = NSUB * 128
NEG = -30000.0


class Buf:
    __slots__ = ("name", "w", "r")

    def __init__(self, name=""):
        self.name = name
        self.w = None
        self.r = {}


class Sched:
    ENG = ("pe", "act", "dve", "pool", "sp")

    def __init__(self, nc, stack):
        self.nc = nc
        self.stack = stack
        self.q = {e: [] for e in self.ENG}
        self.sems = {}
        self.cnt = {}
        self.seen = {e: {} for e in self.ENG}
        self.out_keys = set()
        self.nops = 0

    def sem(self, key):
        if key not in self.sems:
            self.sems[key] = self.stack.enter_context(self.nc.semaphore("s_" + str(key)))
            self.cnt[key] = 0
        return self.sems[key]

    def op(self, eng, fn, reads=(), writes=(), dma=None, is_out=False):
        deps = {}

        def add(k, c):
            if deps.get(k, 0) < c:
                deps[k] = c

        own = eng if dma is None else None
        for b in reads:
            if b.w is not None:
                add(*b.w)
        for b in writes:
            if b.w is not None and b.w[0] != own:
                add(*b.w)
            for k, c in b.r.items():
                if k != own:
                    add(k, c)
        waits = []
        for k, c in deps.items():
            if self.seen[eng].get(k, 0) < c:
                waits.append((k, c))
                self.seen[eng][k] = c
        key = eng if dma is None else dma
        inc = 1 if dma is None else 16
        self.sem(key)
        self.cnt[key] += inc
        c = self.cnt[key]
        self.q[eng].append((waits, fn, key, inc))
        for b in reads:
            if b.r.get(key, 0) < c:
                b.r[key] = c
        for b in writes:
            b.w = (key, c)
            b.r = {}
        if is_out:
            self.out_keys.add(key)
        self.nops += 1

    def barrier(self):
        for e in self.ENG:
            waits = []
            for k, c in self.cnt.items():
                if c > 0 and self.seen[e].get(k, 0) < c:
                    waits.append((k, c))
                    self.seen[e][k] = c
            if waits:
                self.q[e].append((waits, None, None, 0))

    def emit(self):
        nc = self.nc
        fin = [(k, self.cnt[k]) for k in sorted(self.out_keys)]
        sems = self.sems
        q = self.q
        with nc.Block() as block:
            def run(engname, eng):
                for waits, fn, key, inc in q[engname]:
                    for k, c in waits:
                        eng.wait_ge(sems[k], c)
                    if fn is not None:
                        ins = fn(eng)
                        ins.then_inc(sems[key], inc)
                if engname == "sp":
                    for k, c in fin:
                        eng.wait_ge(sems[k], c)

            @block.tensor
            def _(e):
                run("pe", e)

            @block.scalar
            def _(e):
                run("act", e)

            @block.vector
            def _(e):
                run("dve", e)

            @block.gpsimd
            def _(e):
                run("pool", e)

            @block.sync
            def _(e):
                run("sp", e)


class Rot:
    def __init__(self, items):
        self.items = items
        self.i = 0
        self.pinned = set()

    def next(self):
        while True:
            idx = self.i % len(self.items)
            self.i += 1
            if idx not in self.pinned:
                self.last = idx
                return self.items[idx]

    def pin_last(self):
        self.pinned.add(self.last)
        return self.last

    def unpin(self, idx):
        self.pinned.discard(idx)


def build(cfg=None):
    cfg = dict(cfg or {})
    NPRE_RUN = cfg.get('npre', NPREB)
    NFULL_RUN = cfg.get('nfull', NSUBP)
    DO_SAMP = cfg.get('samp', True)
    STAGES = cfg.get('stages', ('mlp0', 'ple0', 'kv', 'swa', 'swas', 'mlp1', 'ple1', 'final'))
    DBG = cfg.get('dbg', False)
    KVF = cfg.get('kvflags', 'abc')
    nc = bass.Bass("TRN2", target_bir_lowering=False)
    D = {}

    def din(name, shape):
        D[name] = nc.dram_tensor(name, list(shape), F32, kind="ExternalInput")
        return D[name].ap()

    def dout(name, shape):
        D[name] = nc.dram_tensor(name, list(shape), F32, kind="ExternalOutput")
        return D[name].ap()

    xpre = din("xpre", [NPREB, 128, 8, 128])
    xmain = din("xmain", [128, 8, T])
    pT = din("pT", [2, 128, 2, T])
    sgla = din("sgla", [16, 4, 128, 256])
    ck = din("ck", [16, 128, 256])
    cv = din("cv", [16, 128, 256])
    ckT = din("ckT", [16, 256, 128])
    w_in = din("w_in", [1024, 3088])
    w_a2a = din("w_a2a", [17, 512])
    w_out = din("w_out", [1024, 1024])
    w_kv = din("w_kv", [1024, 512])
    w_q = din("w_q", [1024, 1024])
    w_o = din("w_o", [1024, 1024])
    w_up = din("w_up", [2, 1024, 4096])
    w_down = din("w_down", [2, 4096, 1024])
    w_ple = din("w_ple", [2, 256, 1024])
    w_pg = din("w_pg", [2, 1024, 1024])
    gains = din("gains", [128, 66])
    sinks_b = din("sinks_b", [128, 16])
    rb_aug = din("rb_aug", [33, 16])
    oh_aug = din("oh_aug", [33, 384])
    flag = din("flag", [128, 1])
    cmask = din("cmask", [6, 128, 128])
    blockmask = din("blockmask", [128, 128])
    sel = din("sel", [128, 16])
    ohsel = din("ohsel", [16, 16, 128])

    yT = dout("yT", [128, 8, 17 * 128])
    st_p = dout("st_p", [4, 128, 256])
    st_s = dout("st_s", [16, 4, 128, 256])
    ckc = dout("ckc", [16, 128, 256])
    cvc = dout("cvc", [16, 128, 256])
    kvo = dout("kvo", [4, 128, 256])
    gd = nc.dram_tensor("gd", [16, 384], F32, kind="Internal")
    g2 = nc.dram_tensor("g2", [16 * 128 * 385 + 512], F32, kind="Internal")

    tok = lambda sb, n=1: slice(128 * sb, 128 * (sb + n))

    with ExitStack() as st0:
        S = Sched(nc, st0)

        _uid = [0]

        def sbt(st, name, shape, dt):
            _uid[0] += 1
            return st.enter_context(nc.sbuf_tensor("%s_u%d" % (name, _uid[0]), list(shape), dt))

        banks = []
        for i in range(8):
            t = st0.enter_context(nc.psum_tensor("pb%d" % i, [128, 512], F32))
            banks.append((t, Buf("pb%d" % i)))
        prot = Rot(banks)

        def mmop(mms, reads, writes):
            def fn(e, mms=mms):
                ins = None
                n = len(mms)
                for i, (o, l, r, stt) in enumerate(mms):
                    ins = e.matmul(o, lhsT=l, rhs=r, start=stt, stop=(i == n - 1), skip_group_check=True)
                return ins
            S.op("pe", fn, reads=reads, writes=writes)

        def act(out, in_, func, reads, writes, scale=None, bias=None):
            kw = {}
            if scale is not None:
                kw["scale"] = scale
            if bias is not None:
                kw["bias"] = bias
            S.op("act", lambda e: e.activation(out=out, in_=in_, func=func, **kw), reads=reads, writes=writes)

        def tt(out, in0, in1, op, reads, writes, eng="dve"):
            S.op(eng, lambda e: e.tensor_tensor(out=out, in0=in0, in1=in1, op=op), reads=reads, writes=writes)

        def stt(out, in0, scalar, in1, op0, op1, reads, writes):
            S.op("dve", lambda e: e.scalar_tensor_tensor(out=out, in0=in0, scalar=scalar, in1=in1, op0=op0, op1=op1),
                 reads=reads, writes=writes)

        def ts(out, in0, s1, op0, reads, writes, s2=None, op1=None, eng="dve"):
            if op1 is None:
                S.op(eng, lambda e: e.tensor_scalar(out=out, in0=in0, scalar1=s1, scalar2=None, op0=op0),
                     reads=reads, writes=writes)
            else:
                S.op(eng, lambda e: e.tensor_scalar(out=out, in0=in0, scalar1=s1, scalar2=s2, op0=op0, op1=op1),
                     reads=reads, writes=writes)

        def cp(eng, out, in_, reads, writes):
            if eng == "act":
                S.op("act", lambda e: e.activation(out=out, in_=in_, func=AF.Copy), reads=reads, writes=writes)
            else:
                S.op(eng, lambda e: e.tensor_copy(out=out, in_=in_), reads=reads, writes=writes)

        def recip(out, in_, reads, writes):
            S.op("dve", lambda e: e.reciprocal(out=out, in_=in_), reads=reads, writes=writes)

        def dma(eng, out, in_, reads, writes, key, is_out=False):
            S.op(eng, lambda e: e.dma_start(out=out, in_=in_), reads=reads, writes=writes, dma=key, is_out=is_out)

        hT = sbt(st0, "hT", [128, 8, T], F32)
        b_h = [Buf("h%d" % i) for i in range(NSUB)]
        gains_sb = sbt(st0, "gains_sb", [128, 66], F32); b_gains = Buf()
        mk = sbt(st0, "mk", [128, 6, 128], BF16); b_mk = Buf()
        caus = sbt(st0, "caus", [128, 2, 128], F32); b_caus = Buf()
        ones = sbt(st0, "ones", [128, 128], BF16); b_ones = Buf()
        sel_sb = sbt(st0, "sel_sb", [128, 16], F32); b_sel = Buf()
        flag_sb = sbt(st0, "flag_sb", [128, 1], F32); b_flag = Buf()
        ES = sbt(st0, "ES", [128, 16], F32); b_ES = Buf()
        wa2 = sbt(st0, "wa2", [32, 512], BF16); b_wa2 = Buf()
        bmask = sbt(st0, "bmask", [128, 128], F32); b_bmask = Buf()

        dma("sp", gains_sb[:], gains, [], [b_gains], "c_gains")
        dma("pool", mk[:], cmask.rearrange("m p i -> p m i"), [], [b_mk], "c_mk")
        dma("sp", caus[:, 0, :], cmask[2], [], [b_caus], "c_caus")
        dma("sp", caus[:, 1, :], cmask[5], [], [b_caus], "c_caus")
        dma("sp", sel_sb[:], sel, [], [b_sel], "c_sel")
        dma("sp", flag_sb[:], flag, [], [b_flag], "c_flag")
        dma("sp", ES[:], sinks_b, [], [b_ES], "c_es")
        dma("sp", bmask[:], blockmask, [], [b_bmask], "c_bm")
        dma("pool", wa2[0:17, :], w_a2a, [], [b_wa2], "c_wa2")
        S.op("dve", lambda e: e.memset(ones[:], 1.0), writes=[b_ones])
        act(ES[:], ES[:], AF.Exp, [b_ES], [b_ES])
        for i in range(0, NSUB, 3):
            n = min(3, NSUB - i)
            dma("sp", hT[:, :, tok(i, n)], xmain[:, :, tok(i, n)], [], b_h[i:i + n], "ld_h%d" % i)

        with ExitStack() as stb:
            rb_sb = sbt(stb, "rb_sb", [33, 16], F32); b_rb = Buf()
            oh_sb = sbt(stb, "oh_sb", [33, 384], F32); b_oh = Buf()
            g_sb = sbt(stb, "g_sb", [16, 384], F32); b_g = Buf()
            dma("sp", rb_sb[:], rb_aug, [], [b_rb], "c_rb")
            dma("sp", oh_sb[:], oh_aug, [], [b_oh], "c_oh")
            pb, b_pb = prot.next()
            mmop([(pb[0:16, 0:384], rb_sb[:, :], oh_sb[:, :], True)], [b_rb, b_oh], [b_pb])
            cp("dve", g_sb[:], pb[0:16, 0:384], [b_pb], [b_g])
            ohs = sbt(stb, "ohs", [16, 16, 128], F32); b_ohs = Buf()
            dma("sp", ohs[:], ohsel, [], [b_ohs], "c_ohs")
            rep_r = Rot([(sbt(stb, "rep%d" % i, [128, 384], F32), Buf(), "c_rep%d" % i) for i in range(2)])
            for h in range(16):
                pr_, b_pr_ = prot.next()
                mmop([(pr_[:, 0:384], ohs[:, h, :], g_sb[:, :], True)], [b_ohs, b_g], [b_pr_])
                rep, b_rep, krep = rep_r.next()
                cp("act" if h % 2 else "dve", rep[:], pr_[:, 0:384], [b_pr_], [b_rep])
                dma("sp", bass.AP(g2, h * 128 * 385, [[385, 128], [1, 384]]), rep[:], [b_rep], [], krep)
            S.barrier()
        BT_SRC = bass.AP(g2, 127, [[384, 128], [128 * 385, 16], [1, 256]])

        def norm_sub(src_ap, b_src, gi, dst_ap, b_dst, W):
            hsq, b_hsq = W["hsq"].next()
            act(hsq[:], src_ap, AF.Square, [b_src], [b_hsq])
            pb, b_pb = prot.next()
            mmop([(pb[:, 0:128], ones[:], hsq[:, kc, :], kc == 0) for kc in range(8)], [b_ones, b_hsq], [b_pb])
            srt, b_srt = W["srt"].next()
            act(srt[:], pb[:, 0:128], AF.Sqrt, [b_pb], [b_srt], scale=1.0 / 1024.0, bias=EPS)
            recip(srt[:], srt[:], [b_srt], [b_srt])
            for kc in range(8):
                stt(dst_ap[:, kc, :], src_ap[:, kc, :], gains_sb[:, gi + kc:gi + kc + 1], srt[:],
                    ALU.mult, ALU.mult, [b_src, b_gains, b_srt], [b_dst])

        with ExitStack() as st1:
            win = sbt(st1, "win", [128, 8, 3088], BF16); b_win = Buf()
            wout = sbt(st1, "wout", [128, 8, 1024], BF16); b_wout = Buf()
            w_in_r = w_in.rearrange("(kc p) n -> p kc n", p=128)
            dma("pool", win[:, :, 512:1024], w_in_r[:, :, 512:1024], [], [b_win], "w_in")
            dma("pool", win[:, :, 1024:2048], w_in_r[:, :, 1024:2048], [], [b_win], "w_in")
            dma("pool", win[:, :, 3072:3088], w_in_r[:, :, 3072:3088], [], [b_win], "w_in")
            Sst = sbt(st1, "Sst", [128, 4, 256], F32); b_S = Buf()
            Sbf = sbt(st1, "Sbf", [128, 4, 256], BF16); b_Sbf = Buf()
            S.op("dve", lambda e: e.memset(Sst[:], 0.0), writes=[b_S])
            S.op("dve", lambda e: e.memset(Sbf[:], 0.0), writes=[b_Sbf])
            aT = sbt(st1, "aT", [32, 128], BF16); b_aT = Buf()
            S.op("dve", lambda e: e.memset(aT[:], 1.0), writes=[b_aT])

            def rot(st, name, shape, dt, n):
                return Rot([(sbt(st, "%s%d" % (name, i), shape, dt), Buf()) for i in range(n)])

            W1 = {
                "hsq": rot(st1, "hsq", [128, 8, 128], BF16, 2),
                "srt": rot(st1, "srt", [128, 128], F32, 2),
            }
            hn_r = rot(st1, "hn", [128, 8, 128], BF16, 2)
            ktok_r = rot(st1, "ktok", [128, 512], BF16, 2)
            vtok_r = rot(st1, "vtok", [128, 1024], BF16, 2)
            t_r = rot(st1, "texp", [128, 512], F32, 1)
            lg_r = rot(st1, "lg", [128, 512], BF16, 2)
            Er_r = rot(st1, "Er", [128, 512], F32, 1)
            kt_r = rot(st1, "kt", [128, 512], BF16, 2)
            Eq_r = rot(st1, "Eq", [128, 4, 128], F32, 2)
            Ek_r = rot(st1, "Ek", [128, 4, 128], F32, 1)
            qT_r = rot(st1, "qT", [128, 4, 128], BF16, 1)
            kT_r = rot(st1, "kTg", [128, 4, 128], BF16, 1)
            sg_r = rot(st1, "sgT", [128, 8, 128], BF16, 1)
            qt_r = rot(st1, "qtT", [128, 4, 128], BF16, 1)
            kh_r = rot(st1, "khT", [128, 4, 128], BF16, 1)
            at_r = rot(st1, "attnT", [128, 4, 128], BF16, 1)
            osq_r = rot(st1, "osq", [128, 8, 128], BF16, 1)
            rs2_r = rot(st1, "rs2", [128, 4, 128], F32, 1)
            sgr_r = rot(st1, "sgr", [128, 8, 128], F32, 1)
            og_r = rot(st1, "ogT", [128, 8, 128], BF16, 1)

            def gla_common(kind, hn, b_hn, mi):
                R = {}
                pa, b_pa = prot.next()
                mmop([(pa[0:16, 0:128], win[:, kc, 3072:3088], hn[:, kc, :], kc == 0) for kc in range(8)],
                     [b_win, b_hn], [b_pa])
                cp("dve", aT[0:16, :], pa[0:16, 0:128], [b_pa], [b_aT])
                pk, b_pk = prot.next()
                mmop([(pk[:, :], hn[:, kc, :], win[:, kc, 512:1024], kc == 0) for kc in range(8)],
                     [b_win, b_hn], [b_pk])
                ktok, b_ktok = ktok_r.next()
                cp("dve", ktok[:], pk[:, :], [b_pk], [b_ktok])
                vtok, b_vtok = vtok_r.next()
                for half in range(2):
                    pv, b_pv = prot.next()
                    mmop([(pv[:, :], hn[:, kc, :], win[:, kc, 1024 + 512 * half:1536 + 512 * half], kc == 0)
                          for kc in range(8)], [b_win, b_hn], [b_pv])
                    cp("act", vtok[:, 512 * half:512 * half + 512], pv[:, :], [b_pv], [b_vtok])
                pz, b_pz = prot.next()
                mmop([(pz[:, :], aT[0:17, :], wa2[0:17, :], True)], [b_aT, b_wa2], [b_pz])
                tex, b_tex = t_r.next()
                act(tex[:], pz[:, :], AF.Exp, [b_pz], [b_tex], scale=-1.0)
                lg, b_lg = lg_r.next()
                act(lg[:], tex[:], AF.Ln, [b_tex], [b_lg], bias=1.0)
                pr, b_pr = prot.next()
                mmop([(pr[:, :], mk[:, mi + 1, :], lg[:], True)], [b_mk, b_lg], [b_pr])
                Er, b_Er = Er_r.next()
                act(Er[:], pr[:, :], AF.Exp, [b_pr], [b_Er])
                kt, b_kt = kt_r.next()
                tt(kt[:], ktok[:], Er[:], ALU.mult, [b_ktok, b_Er], [b_kt])
                R.update(vtok=vtok, b_vtok=b_vtok, lg=lg, b_lg=b_lg, kt=kt, b_kt=b_kt)
                return R

            def state_update(R, Eq_cols, b_Eq):
                for hh in range(2):
                    pu, b_pu = prot.next()
                    mms = []
                    for h2 in range(2):
                        h = 2 * hh + h2
                        mms.append((pu[:, 256 * h2:256 * h2 + 256], R["kt"][:, 128 * h:128 * h + 128],
                                    R["vtok"][:, 256 * h:256 * h + 256], h2 == 0))
                    mmop(mms, [R["b_kt"], R["b_vtok"]], [b_pu])
                    for h2 in range(2):
                        h = 2 * hh + h2
                        stt(Sst[:, h, :], Sst[:, h, :], Eq_cols(h), pu[:, 256 * h2:256 * h2 + 256],
                            ALU.mult, ALU.add, [b_S, b_Eq, b_pu], [b_S])
                cp("act", Sbf[:], Sst[:], [b_S], [b_Sbf])

            with ExitStack() as st1a:
                xs_r = rot(st1a, "xs", [128, 8, 128], F32, 2)
                Eqp_r = rot(st1a, "Eqp", [128, 4], F32, 2)
                for pbk in range(NPRE_RUN):
                    xs, b_xs = xs_r.next()
                    dma("sp", xs[:], xpre[pbk], [], [b_xs], "ld_xs%d" % (pbk % 2))
                    hn, b_hn = hn_r.next()
                    norm_sub(xs[:], b_xs, 0, hn, b_hn, W1)
                    R = gla_common("pre", hn, b_hn, 0)
                    pc, b_pc = prot.next()
                    mmop([(pc[:, h:h + 1], R["lg"][:, 128 * h:128 * h + 128], mk[:, 0, 127:128], h == 0)
                          for h in range(4)], [R["b_lg"], b_mk], [b_pc])
                    Eqp, b_Eqp = Eqp_r.next()
                    act(Eqp[:], pc[:, 0:4], AF.Exp, [b_pc], [b_Eqp])
                    state_update(R, lambda h, Eqp=Eqp: Eqp[:, h:h + 1], b_Eqp)
                S.barrier()

            dma("pool", win[:, :, 0:512], w_in_r[:, :, 0:512], [], [b_win], "w_in")
            dma("pool", win[:, :, 2048:3072], w_in_r[:, :, 2048:3072], [], [b_win], "w_in")
            dma("pool", wout[:], w_out.rearrange("(kc p) n -> p kc n", p=128), [], [b_wout], "w_out")

            def gla_full(sb, samp, SX=None):
                mi = 3 if samp else 0
                ci = 1 if samp else 0
                hn, b_hn = hn_r.next()
                norm_sub(hT[:, :, tok(sb)], b_h[sb], 0, hn, b_hn, W1)
                pq, b_pq = prot.next()
                mmop([(pq[:, 128 * h:128 * h + 128], win[:, kc, 128 * h:128 * h + 128], hn[:, kc, :], (h == 0 and kc == 0))
                      for h in range(4) for kc in range(8)], [b_win, b_hn], [b_pq])
                qT, b_qT = qT_r.next()
                act(qT[:], pq[:, :].rearrange("p (a b) -> p a b", a=4), AF.Copy, [b_pq], [b_qT], scale=128.0 ** -0.5)
                pk, b_pk = prot.next()
                mmop([(pk[:, 128 * h:128 * h + 128], win[:, kc, 512 + 128 * h:512 + 128 * h + 128], hn[:, kc, :],
                       (h == 0 and kc == 0)) for h in range(4) for kc in range(8)], [b_win, b_hn], [b_pk])
                kTg, b_kTg = kT_r.next()
                cp("dve", kTg[:], pk[:, :].rearrange("p (a b) -> p a b", a=4), [b_pk], [b_kTg])
                sgT, b_sgT = sg_r.next()
                for half in range(2):
                    pg, b_pg = prot.next()
                    mmop([(pg[:, 128 * j:128 * j + 128], win[:, kc, 2048 + 512 * half + 128 * j:2048 + 512 * half + 128 * j + 128],
                           hn[:, kc, :], (j == 0 and kc == 0)) for j in range(4) for kc in range(8)],
                         [b_win, b_hn], [b_pg])
                    act(sgT[:, 4 * half:4 * half + 4, :], pg[:, :].rearrange("p (a b) -> p a b", a=4), AF.Silu,
                        [b_pg], [b_sgT])
                R = gla_common("full", hn, b_hn, mi)
                pbc, b_pbc = prot.next()
                mmop([(pbc[:, 128 * h:128 * h + 128], R["lg"][:, 128 * h:128 * h + 128], mk[:, mi, :], h == 0)
                      for h in range(4)], [R["b_lg"], b_mk], [b_pbc])
                Eq, b_Eq = Eq_r.next()
                act(Eq[:], pbc[:, :].rearrange("p (a b) -> p a b", a=4), AF.Exp, [b_pbc], [b_Eq])
                Ek, b_Ek = Ek_r.next()
                act(Ek[:], pbc[:, :].rearrange("p (a b) -> p a b", a=4), AF.Exp, [b_pbc], [b_Ek], scale=-1.0)
                qtT, b_qt = qt_r.next()
                tt(qtT[:], qT[:], Eq[:], ALU.mult, [b_qT, b_Eq], [b_qt])
                khT, b_kh = kh_r.next()
                tt(khT[:], kTg[:], Ek[:], ALU.mult, [b_kTg, b_Ek], [b_kh])
                pat, b_pat = prot.next()
                mmop([(pat[:, 128 * h:128 * h + 128], khT[:, h, :], qtT[:, h, :], h == 0) for h in range(4)],
                     [b_kh, b_qt], [b_pat])
                atT, b_at = at_r.next()
                tt(atT[:], pat[:, :].rearrange("p (a b) -> p a b", a=4),
                   caus[:, ci, :].unsqueeze(1).to_broadcast([128, 4, 128]), ALU.mult, [b_pat, b_caus], [b_at])
                pos = []
                pins = []
                for hh in range(2):
                    po, b_po = prot.next()
                    pins.append(prot.pin_last())
                    pos.append((po, b_po))
                    mms = []
                    first = True
                    for h2 in range(2):
                        h = 2 * hh + h2
                        for c in range(2):
                            o_ap = po[:, 128 * (2 * h2 + c):128 * (2 * h2 + c) + 128]
                            mms.append((o_ap, R["vtok"][:, 256 * h + 128 * c:256 * h + 128 * c + 128], atT[:, h, :], first))
                            first = False
                            if not samp:
                                mms.append((o_ap, Sbf[:, h, 128 * c:128 * c + 128], qtT[:, h, :], False))
                    rd = [R["b_vtok"], b_at, b_qt] + ([] if samp else [b_Sbf])
                    mmop(mms, rd, [b_po])
                if not samp:
                    state_update(R, lambda h, Eq=Eq: Eq[:, h, 127:128], b_Eq)
                else:
                    for sq in range(16):
                        s0f, b_s0f = SX["s0f"].next()
                        dma("sp", s0f[:], sgla[sq].rearrange("h k v -> k h v"), [], [b_s0f], "ld_s0_%d" % (sq % 2))
                        s0b, b_s0b = SX["s0b"].next()
                        cp("act", s0b[:], s0f[:], [b_s0f], [b_s0b])
                        for hh in range(2):
                            po, b_po = pos[hh]
                            mms = []
                            for h2 in range(2):
                                h = 2 * hh + h2
                                for c in range(2):
                                    col = 128 * (2 * h2 + c) + 8 * sq
                                    mms.append((po[:, col:col + 8], s0b[:, h, 128 * c:128 * c + 128],
                                                qtT[:, h, 8 * sq:8 * sq + 8], False))
                            mmop(mms, [b_s0b, b_qt], [b_po])
                        ktm, b_ktm = SX["ktm"].next()
                        ts(ktm[:], R["kt"][:], sel_sb[:, sq:sq + 1], ALU.mult, [R["b_kt"], b_sel], [b_ktm])
                        for hh in range(2):
                            pu, b_pu = prot.next()
                            mms = []
                            for h2 in range(2):
                                h = 2 * hh + h2
                                mms.append((pu[:, 256 * h2:256 * h2 + 256], ktm[:, 128 * h:128 * h + 128],
                                            R["vtok"][:, 256 * h:256 * h + 256], h2 == 0))
                            mmop(mms, [b_ktm, R["b_vtok"]], [b_pu])
                            for h2 in range(2):
                                h = 2 * hh + h2
                                stt(s0f[:, h, :], s0f[:, h, :], Eq[:, h, 8 * sq + 7:8 * sq + 8],
                                    pu[:, 256 * h2:256 * h2 + 256], ALU.mult, ALU.add, [b_s0f, b_Eq, b_pu], [b_s0f])
                        dma("sp", st_s[sq].rearrange("h k v -> k h v"), s0f[:], [b_s0f], [], "st_s0_%d" % (sq % 2),
                            is_out=True)
                osq, b_osq = osq_r.next()
                for hh in range(2):
                    po, b_po = pos[hh]
                    act(osq[:, 4 * hh:4 * hh + 4, :], po[:, :].rearrange("p (a b) -> p a b", a=4), AF.Square,
                        [b_po], [b_osq])
                pss, b_pss = prot.next()
                mmop([(pss[:, 128 * h:128 * h + 128], ones[:], osq[:, 2 * h + c, :], (h == 0 and c == 0))
                      for h in range(4) for c in range(2)], [b_ones, b_osq], [b_pss])
                rs2, b_rs2 = rs2_r.next()
                act(rs2[:], pss[:, :].rearrange("p (a b) -> p a b", a=4), AF.Sqrt, [b_pss], [b_rs2],
                    scale=1.0 / 256.0, bias=EPS)
                recip(rs2[:], rs2[:], [b_rs2], [b_rs2])
                sgr, b_sgr = sgr_r.next()
                tt(sgr[:].rearrange("p (h c) t -> p h c t", c=2), sgT[:].rearrange("p (h c) t -> p h c t", c=2),
                   rs2[:].unsqueeze(2).to_broadcast([128, 4, 2, 128]), ALU.mult, [b_sgT, b_rs2], [b_sgr])
                ogT, b_og = og_r.next()
                for j in range(8):
                    po, b_po = pos[j // 4]
                    stt(ogT[:, j, :], po[:, 128 * (j % 4):128 * (j % 4) + 128], gains_sb[:, 64 + (j % 2):65 + (j % 2)],
                        sgr[:, j, :], ALU.mult, ALU.mult, [b_po, b_gains, b_sgr], [b_og])
                for pi in pins:
                    prot.unpin(pi)
                for half in range(2):
                    py, b_py = prot.next()
                    mmop([(py[:, 128 * oc:128 * oc + 128], wout[:, j, 128 * (4 * half + oc):128 * (4 * half + oc) + 128],
                           ogT[:, j, :], (oc == 0 and j == 0)) for oc in range(4) for j in range(8)],
                         [b_wout, b_og], [b_py])
                    tt(hT[:, 4 * half:4 * half + 4, tok(sb)], py[:, :].rearrange("p (a b) -> p a b", a=4),
                       hT[:, 4 * half:4 * half + 4, tok(sb)], ALU.add, [b_py, b_h[sb]], [b_h[sb]])

            for sb in range(NFULL_RUN):
                gla_full(sb, False)
            dma("sp", st_p.rearrange("h k v -> k h v"), Sst[:], [b_S], [], "st_stp", is_out=True)
            with ExitStack() as st1b:
                SX = {
                    "s0f": rot(st1b, "s0f", [128, 4, 256], F32, 2),
                    "s0b": rot(st1b, "s0b", [128, 4, 256], BF16, 2),
                    "ktm": rot(st1b, "ktm", [128, 512], BF16, 2),
                }
                if DO_SAMP:
                    gla_full(NSUB - 1, True, SX)
                S.barrier()
            S.barrier()

        def token_blocks(sb0):
            blks = []
            s = sb0
            while s < NSUB:
                n = min(4, NSUB - s)
                blks.append((s, n))
                s += n
            return blks

        def norm_all(st, sb0, gi, hnT, b_hn):
            W = {
                "hsq": Rot([(sbt(st, "hsqA%d" % i, [128, 8, 128], BF16), Buf()) for i in range(2)]),
                "srt": Rot([(sbt(st, "srtA%d" % i, [128, 128], F32), Buf()) for i in range(2)]),
            }
            for sb in range(sb0, NSUB):
                norm_sub(hT[:, :, tok(sb)], b_h[sb], gi, hnT[:, :, tok(sb)], b_hn[sb], W)

        def mlp_stage(l, sb0):
            with ExitStack() as st:
                hnT = sbt(st, "hnT", [128, 8, T], BF16)
                b_hn = [Buf() for _ in range(NSUB)]
                wu_r = Rot([(sbt(st, "wu%d" % i, [128, 8, 512], BF16), Buf(), "wu%d" % i) for i in range(2)])
                wd_r = Rot([(sbt(st, "wd%d" % i, [128, 4, 1024], BF16), Buf(), "wd%d" % i) for i in range(2)])
                rl_r = Rot([(sbt(st, "rl%d" % i, [128, 512], F32), Buf()) for i in range(2)])
                ac_r = Rot([(sbt(st, "ac%d" % i, [128, 4, 512], BF16), Buf()) for i in range(2)])
                wu_src = w_up[l].rearrange("(kc p) n -> p kc n", p=128)
                wd_src = w_down[l].rearrange("(kc p) n -> p kc n", p=128)

                def load(g):
                    wu, b_wu, kwu = wu_r.next()
                    wd, b_wd, kwd = wd_r.next()
                    dma("pool", wu[:], wu_src[:, :, 512 * g:512 * g + 512], [], [b_wu], kwu)
                    dma("pool", wd[:], wd_src[:, 4 * g:4 * g + 4, :], [], [b_wd], kwd)
                    return (wu, b_wu, wd, b_wd)

                loaded = {0: load(0)}
                norm_all(st, sb0, 16 + 8 * l, hnT, b_hn)
                blks = token_blocks(sb0)
                for g in range(8):
                    if g + 1 < 8:
                        loaded[g + 1] = load(g + 1)
                    wu, b_wu, wd, b_wd = loaded.pop(g)
                    for (s, n) in blks:
                        N = 128 * n
                        cols = tok(s, n)
                        ac, b_ac = ac_r.next()
                        for fc in range(4):
                            pu, b_pu = prot.next()
                            mmop([(pu[:, 0:N], wu[:, kc, 128 * fc:128 * fc + 128], hnT[:, kc, cols], kc == 0)
                                  for kc in range(8)], [b_wu] + b_hn[s:s + n], [b_pu])
                            rl, b_rl = rl_r.next()
                            act(rl[:, 0:N], pu[:, 0:N], AF.Relu, [b_pu], [b_rl])
                            act(ac[:, fc, 0:N], rl[:, 0:N], AF.Square, [b_rl], [b_ac])
                        for oc in range(8):
                            pd, b_pd = prot.next()
                            mmop([(pd[:, 0:N], wd[:, fc, 128 * oc:128 * oc + 128], ac[:, fc, 0:N], fc == 0)
                                  for fc in range(4)], [b_wd, b_ac], [b_pd])
                            tt(hT[:, oc, cols], pd[:, 0:N], hT[:, oc, cols], ALU.add, [b_pd] + b_h[s:s + n], b_h[s:s + n])
                S.barrier()

        def ple_stage(l, sb0):
            with ExitStack() as st:
                hnT = sbt(st, "hnTp", [128, 8, T], BF16)
                b_hn = [Buf() for _ in range(NSUB)]
                wpg = sbt(st, "wpg", [128, 8, 1024], BF16); b_wpg = Buf()
                wpl = sbt(st, "wpl", [128, 2, 1024], BF16); b_wpl = Buf()
                pTs = sbt(st, "pTs", [128, 2, T], BF16); b_pTs = Buf()
                sg_r2 = Rot([(sbt(st, "sgp%d" % i, [128, 512], F32), Buf()) for i in range(2)])
                tm_r = Rot([(sbt(st, "tmp%d" % i, [128, 512], F32), Buf()) for i in range(2)])
                dma("pool", wpg[:], w_pg[l].rearrange("(kc p) n -> p kc n", p=128), [], [b_wpg], "wpg")
                dma("pool", wpl[:], w_ple[l].rearrange("(kc p) n -> p kc n", p=128), [], [b_wpl], "wpl")
                for hc in range(2):
                    dma("pool", pTs[:, :, 1152 * hc:1152 * hc + 1152], pT[l][:, :, 1152 * hc:1152 * hc + 1152],
                        [], [b_pTs], "pTs")
                norm_all(st, sb0, 32 + 8 * l, hnT, b_hn)
                for (s, n) in token_blocks(sb0):
                    N = 128 * n
                    cols = tok(s, n)
                    for oc in range(8):
                        pg, b_pg = prot.next()
                        mmop([(pg[:, 0:N], wpg[:, kc, 128 * oc:128 * oc + 128], hnT[:, kc, cols], kc == 0)
                              for kc in range(8)], [b_wpg] + b_hn[s:s + n], [b_pg])
                        pe_, b_pe = prot.next()
                        mmop([(pe_[:, 0:N], wpl[:, kc, 128 * oc:128 * oc + 128], pTs[:, kc, cols], kc == 0)
                              for kc in range(2)], [b_wpl, b_pTs], [b_pe])
                        sg, b_sg = sg_r2.next()
                        act(sg[:, 0:N], pg[:, 0:N], AF.Sigmoid, [b_pg], [b_sg])
                        tm, b_tm = tm_r.next()
                        tt(tm[:, 0:N], pe_[:, 0:N], sg[:, 0:N], ALU.mult, [b_pe, b_sg], [b_tm])
                        tt(hT[:, oc, cols], tm[:, 0:N], hT[:, oc, cols], ALU.add, [b_tm] + b_h[s:s + n], b_h[s:s + n])
                S.barrier()

        if 'mlp0' in STAGES:
            mlp_stage(0, 0)
        if 'ple0' in STAGES:
            ple_stage(0, 0)

        with ExitStack() as stkv:
            kTs = sbt(stkv, "kTs", [128, 2, T], BF16); b_kTs = [Buf() for _ in range(NSUB)]
            Vt = sbt(stkv, "Vt", [128, NSUB, 256], BF16); b_Vt = [Buf() for _ in range(NSUB)]
            if 'kv' in STAGES:
                with ExitStack() as st:
                    hnT = sbt(st, "hnTk", [128, 8, T], BF16)
                    b_hn = [Buf() for _ in range(NSUB)]
                    wkv = sbt(st, "wkv", [128, 8, 512], BF16); b_wkv = Buf()
                    kk_r = Rot([(sbt(st, "kk%d" % i, [128, 256], F32), Buf(), "st_kk%d" % i) for i in range(2)])
                    vv_r = Rot([(sbt(st, "vv%d" % i, [128, 256], F32), Buf(), "st_vv%d" % i) for i in range(2)])
                    dma("pool", wkv[:], w_kv.rearrange("(kc p) n -> p kc n", p=128), [], [b_wkv], "wkv")
                    cb_r = Rot([(sbt(st, "cb%d" % i, [128, 4, 256], F32), Buf(), "ld_cb%d" % i, "st_cb%d" % i) for i in range(2)])
                    for src_, dst_ in ((ck, ckc), (cv, cvc)):
                        for g4 in range(4):
                            cb, b_cb, kld, kst = cb_r.next()
                            dma("sp", cb[:], src_[4 * g4:4 * g4 + 4].rearrange("s k f -> k s f"), [], [b_cb], kld)
                            dma("sp", dst_[4 * g4:4 * g4 + 4].rearrange("s k f -> k s f"), cb[:], [b_cb], [], kst, is_out=True)
                    kvo_sb = sbt(st, "kvo_sb", [128, 4, 256], F32); b_kvo = Buf()
                    norm_all(st, 0, 48, hnT, b_hn)
                    for (s, n) in token_blocks(0):
                        N = 128 * n
                        cols = tok(s, n)
                        for c in range(2):
                            pk, b_pk = prot.next()
                            mmop([(pk[:, 0:N], wkv[:, kc, 128 * c:128 * c + 128], hnT[:, kc, cols], kc == 0)
                                  for kc in range(8)], [b_wkv] + b_hn[s:s + n], [b_pk])
                            cp("act" if c == 0 else "dve", kTs[:, c, cols], pk[:, 0:N], [b_pk], b_kTs[s:s + n])
                    for sb in range(NSUB):
                        pv, b_pv = prot.next()
                        mmop([(pv[:, :], hnT[:, kc, tok(sb)], wkv[:, kc, :], kc == 0) for kc in range(8)],
                             [b_wkv, b_hn[sb]], [b_pv])
                        cp("act", Vt[:, sb, :], pv[:, 256:512], [b_pv], [b_Vt[sb]])
                        if sb >= NSUB - 2:
                            o2 = 0 if sb == NSUB - 2 else 2
                            cp("act", kvo_sb[:, o2:o2 + 2, :], pv[:, :].rearrange("p (a b) -> p a b", a=2), [b_pv], [b_kvo])
                    dma("sp", kvo.rearrange("h k v -> k h v"), kvo_sb[:], [b_kvo], [], "st_kvo", is_out=True)
                    S.barrier()

            if 'swa' in STAGES:
                with ExitStack() as st:
                    wq = sbt(st, "wq", [128, 8, 1024], BF16); b_wq = Buf()
                    wo = sbt(st, "wo", [128, 8, 1024], BF16); b_wo = Buf()
                    Bt = sbt(st, "Bt", [128, 16, 256], F32); b_Bt = Buf()
                    dma("pool", wq[:], w_q.rearrange("(kc p) n -> p kc n", p=128), [], [b_wq], "wq")
                    dma("pool", wo[:], w_o.rearrange("(kc p) n -> p kc n", p=128), [], [b_wo], "wo")
                    dma("sp", Bt[:], BT_SRC, [], [b_Bt], "ld_bt")
                    sf_r = Rot([(sbt(st, "sf%d" % i, [128, 512], F32), Buf()) for i in range(2)])
                    P_r = Rot([(sbt(st, "Pm%d" % i, [128, 512], BF16), Buf()) for i in range(4)])
                    dn_r = Rot([(sbt(st, "dn%d" % i, [128, 512], F32), Buf()) for i in range(2)])
                    Wn = {
                        "hsq": Rot([(sbt(st, "hsqS%d" % i, [128, 8, 128], BF16), Buf()) for i in range(2)]),
                        "srt": Rot([(sbt(st, "srtS%d" % i, [128, 128], F32), Buf()) for i in range(2)]),
                    }

                    def finish_head(kv, po, b_po, pdn, b_pdn, OT, b_OT, qc, nq):
                        pbase = 64 * (kv % 2)
                        dn, b_dn = dn_r.next()
                        for g in range(4):
                            ts(dn[pbase:pbase + 64, nq * g:nq * g + nq], pdn[pbase:pbase + 64, nq * g:nq * g + nq],
                               ES[pbase:pbase + 64, 4 * kv + g:4 * kv + g + 1], ALU.add, [b_pdn, b_ES], [b_dn])
                        recip(dn[pbase:pbase + 64, 0:4 * nq], dn[pbase:pbase + 64, 0:4 * nq], [b_dn], [b_dn])
                        c0 = 4 * (kv // 2)
                        tt(OT[pbase:pbase + 64, c0:c0 + 4, qc],
                           po[pbase:pbase + 64, 0:4 * nq].rearrange("p (a b) -> p a b", a=4),
                           dn[pbase:pbase + 64, 0:4 * nq].rearrange("p (a b) -> p a b", a=4), ALU.mult,
                           [b_po, b_dn], [b_OT])

                    def wo_resid(OT, b_OT, s, n):
                        N = 128 * n
                        cols = tok(s, n)
                        for oc in range(8):
                            py, b_py = prot.next()
                            mmop([(py[:, 0:N], wo[:, j, 128 * oc:128 * oc + 128], OT[:, j, 0:N], j == 0) for j in range(8)],
                                 [b_wo, b_OT], [b_py])
                            tt(hT[:, oc, cols], py[:, 0:N], hT[:, oc, cols], ALU.add, [b_py] + b_h[s:s + n], b_h[s:s + n])

                    with ExitStack() as stm:
                        hnb = sbt(stm, "hnb", [128, 8, 512], BF16); b_hnb = [Buf() for _ in range(4)]
                        qTb = sbt(stm, "qTb", [128, 8, 512], BF16); b_qTb = Buf()
                        OTb = sbt(stm, "OTb", [128, 8, 512], BF16); b_OTb = Buf()
                        for blk in range(4):
                            s0 = 1 + 4 * blk
                            for i in range(4):
                                norm_sub(hT[:, :, tok(s0 + i)], b_h[s0 + i], 8, hnb[:, :, tok(i)], b_hnb[i], Wn)
                            for j in range(8):
                                pq, b_pq = prot.next()
                                mmop([(pq[:, :], wq[:, kc, 128 * j:128 * j + 128], hnb[:, kc, :], kc == 0) for kc in range(8)],
                                     [b_wq] + b_hnb, [b_pq])
                                cp("act" if j % 2 == 0 else "dve", qTb[:, j, :], pq[:, :], [b_pq], [b_qTb])
                            for i in range(4):
                                sb = s0 + i
                                qc = tok(i)
                                for kv in range(4):
                                    pbase = 64 * (kv % 2)
                                    kch = kv // 2
                                    c0 = 4 * (kv // 2)
                                    Ps = []
                                    for side in range(2):
                                        ksb = sb - side
                                        ps_, b_ps = prot.next()
                                        mmop([(ps_[:, :].rearrange("p (a b) -> p a b", a=4),
                                               kTs[pbase:pbase + 64, kch, tok(ksb)], qTb[pbase:pbase + 64, c0:c0 + 4, qc], True)],
                                             [b_kTs[ksb], b_qTb], [b_ps])
                                        sf, b_sf = sf_r.next()
                                        stt(sf[:].rearrange("p (a b) -> p a b", a=4), ps_[:, :].rearrange("p (a b) -> p a b", a=4),
                                            0.125, Bt[:, 4 * kv:4 * kv + 4, 128 * side:128 * side + 128], ALU.mult, ALU.add,
                                            [b_ps, b_Bt], [b_sf])
                                        Pm, b_Pm = P_r.next()
                                        act(Pm[:], sf[:], AF.Exp, [b_sf], [b_Pm])
                                        if blk == 0 and i == 0 and side == 1:
                                            ts(Pm[:], Pm[:], flag_sb[:, 0:1], ALU.mult, [b_Pm, b_flag], [b_Pm])
                                        Ps.append((Pm, b_Pm, ksb))
                                    po, b_po = prot.next()
                                    mmop([(po[pbase:pbase + 64, :], Vt[:, ksb, 64 * kv:64 * kv + 64], Pm[:], si == 0)
                                          for si, (Pm, b_Pm, ksb) in enumerate(Ps)],
                                         [Ps[0][1], Ps[1][1], b_Vt[Ps[0][2]], b_Vt[Ps[1][2]]], [b_po])
                                    pdn, b_pdn = prot.next()
                                    mmop([(pdn[pbase:pbase + 64, :], ones[:, 0:64], Pm[:], si == 0)
                                          for si, (Pm, b_Pm, ksb) in enumerate(Ps)], [Ps[0][1], Ps[1][1], b_ones], [b_pdn])
                                    finish_head(kv, po, b_po, pdn, b_pdn, OTb, b_OTb, qc, 128)
                            wo_resid(OTb, b_OTb, s0, 4)
                        S.barrier()

                    with ExitStack() as sts:
                      if 'swas' in STAGES:
                        sbS = NSUB - 1
                        hns = sbt(sts, "hns", [128, 8, 128], BF16); b_hns = Buf()
                        qTs = sbt(sts, "qTs", [128, 8, 128], BF16); b_qTs = Buf()
                        OTs = sbt(sts, "OTs", [128, 8, 128], BF16); b_OTs = Buf()
                        Bnew = sbt(sts, "Bnew", [128, 16, 128], F32); b_Bnew = Buf()
                        Bp8 = sbt(sts, "Bp8", [128, 16, 8], F32); b_Bp8 = Buf()
                        ckTs = sbt(sts, "ckTs", [128, 2, 16, 128], BF16); b_ckTs = Buf()
                        cvs_sb = sbt(sts, "cvs_sb", [128, 16, 256], BF16); b_cvsb = Buf()
                        for cc in range(2):
                            dma("pool", ckTs[:, cc], ckT[:, 128 * cc:128 * cc + 128, :].rearrange("s p k -> p s k"),
                                [], [b_ckTs], "ld_ckT")
                        dma("pool", cvs_sb[:], cv.rearrange("s k f -> k s f"), [], [b_cvsb], "ld_cvs")
                        tt(Bnew[:], Bt[:, :, 0:128], bmask[:].unsqueeze(1).to_broadcast([128, 16, 128]), ALU.add,
                           [b_Bt, b_bmask], [b_Bnew])
                        ts(Bp8[:], Bt[:, :, 128:136], 8.0, ALU.mult, [b_Bt], [b_Bp8])
                        norm_sub(hT[:, :, tok(sbS)], b_h[sbS], 8, hns, b_hns, Wn)
                        for half in range(2):
                            pq, b_pq = prot.next()
                            mmop([(pq[:, 128 * j:128 * j + 128], wq[:, kc, 128 * (4 * half + j):128 * (4 * half + j) + 128],
                                   hns[:, kc, :], (j == 0 and kc == 0)) for j in range(4) for kc in range(8)],
                                 [b_wq, b_hns], [b_pq])
                            cp("act", qTs[:, 4 * half:4 * half + 4, :], pq[:, :].rearrange("p (a b) -> p a b", a=4),
                               [b_pq], [b_qTs])
                        for kv in range(4):
                            pbase = 64 * (kv % 2)
                            kch = kv // 2
                            c0 = 4 * (kv // 2)
                            ps_, b_ps = prot.next()
                            mmop([(ps_[:, :].rearrange("p (a b) -> p a b", a=4), kTs[pbase:pbase + 64, kch, tok(sbS)],
                                   qTs[pbase:pbase + 64, c0:c0 + 4, :], True)], [b_kTs[sbS], b_qTs], [b_ps])
                            sf, b_sf = sf_r.next()
                            stt(sf[:].rearrange("p (a b) -> p a b", a=4), ps_[:, :].rearrange("p (a b) -> p a b", a=4), 0.125,
                                Bnew[:, 4 * kv:4 * kv + 4, :], ALU.mult, ALU.add, [b_ps, b_Bnew], [b_sf])
                            Pn, b_Pn = P_r.next()
                            act(Pn[:], sf[:], AF.Exp, [b_sf], [b_Pn])
                            pp, b_pp = prot.next()
                            mmop([(pp[:, 32 * sq:32 * sq + 32].rearrange("p (a b) -> p a b", a=4),
                                   ckTs[pbase:pbase + 64, kch, sq, :], qTs[pbase:pbase + 64, c0:c0 + 4, 8 * sq:8 * sq + 8], sq == 0)
                                  for sq in range(16)], [b_ckTs, b_qTs], [b_pp])
                            sf2, b_sf2 = sf_r.next()
                            tt(sf2[:].rearrange("p (s a b) -> p s a b", s=16, a=4),
                               pp[:, :].rearrange("p (s a b) -> p s a b", s=16, a=4),
                               Bp8[:, 4 * kv:4 * kv + 4, :].unsqueeze(1).to_broadcast([128, 16, 4, 8]), ALU.add,
                               [b_pp, b_Bp8], [b_sf2])
                            Pp, b_Pp = P_r.next()
                            act(Pp[:], sf2[:], AF.Exp, [b_sf2], [b_Pp], scale=0.125)
                            po, b_po = prot.next()
                            pdn, b_pdn = prot.next()
                            for (pz_, b_pz_, isden) in ((po, b_po, False), (pdn, b_pdn, True)):
                                mms = [(pz_[pbase:pbase + 64, :], ones[:, 0:64] if isden else Vt[:, sbS, 64 * kv:64 * kv + 64],
                                        Pn[:], True)]
                                for sq in range(16):
                                    for g in range(4):
                                        o_ap = pz_[pbase:pbase + 64, 128 * g + 8 * sq:128 * g + 8 * sq + 8]
                                        mms.append((o_ap, ones[:, 0:64] if isden else cvs_sb[:, sq, 64 * kv:64 * kv + 64],
                                                    Pp[:, 32 * sq + 8 * g:32 * sq + 8 * g + 8], False))
                                mmop(mms, [b_Pn, b_Pp, b_ones, b_Vt[sbS], b_cvsb], [b_pz_])
                            finish_head(kv, po, b_po, pdn, b_pdn, OTs, b_OTs, slice(0, 128), 128)
                        wo_resid(OTs, b_OTs, sbS, 1)
                        S.barrier()
                    S.barrier()

            if 'mlp1' in STAGES:
                mlp_stage(1, 1)
            if 'ple1' in STAGES:
                ple_stage(1, 1)

            if 'final' in STAGES:
                with ExitStack() as st:
                    Wn = {
                        "hsq": Rot([(sbt(st, "hsqF%d" % i, [128, 8, 128], BF16), Buf()) for i in range(2)]),
                        "srt": Rot([(sbt(st, "srtF%d" % i, [128, 128], F32), Buf()) for i in range(2)]),
                    }
                    yo_r = Rot([(sbt(st, "yo%d" % i, [128, 8, 128], F32), Buf(), "st_y%d" % i) for i in range(3)])
                    for sb in range(1, NSUB):
                        yo, b_yo, key = yo_r.next()
                        norm_sub(hT[:, :, tok(sb)], b_h[sb], 56, yo, b_yo, Wn)
                        dma("sp", yT[:, :, tok(sb - 1)], yo[:], [b_yo], [], key, is_out=True)
                    S.barrier()
        if DBG:
            dbgh = dout("dbgh", [128, 8, T])
            S.barrier()
            dma("sp", dbgh, hT[:], b_h, [], "st_dbg", is_out=True)
        S.emit()
    return nc, S


def _rel_bucket_np(d):
    import math
    max_exact = 16
    df = np.maximum(d, 1).astype(np.float32)
    large = max_exact + (np.log(df / max_exact) / math.log(128 / max_exact) * (32 - max_exact)).astype(np.int32)
    large = np.minimum(large, 31)
    return np.where(d < max_exact, d, large)


def _consts():
    j = np.arange(128)[:, None]
    i = np.arange(128)[None, :]
    same = (j // 8) == (i // 8)
    cm = np.zeros((6, 128, 128), np.float32)
    cm[0] = np.where(j <= i, -1.0 / 16, 0.0)
    cm[1] = np.where(j > i, -1.0 / 16, 0.0)
    cm[2] = np.where(j <= i, 1.0, 0.0)
    cm[3] = np.where(same & (j <= i), -1.0 / 16, 0.0)
    cm[4] = np.where(same & (j > i), -1.0 / 16, 0.0)
    cm[5] = np.where(same & (j <= i), 1.0, 0.0)
    blockmask = np.where(same, 0.0, NEG).astype(np.float32)
    sel = (np.arange(128)[:, None] // 8 == np.arange(16)[None, :]).astype(np.float32)
    oh = np.zeros((33, 384), np.float32)
    for n in range(384):
        d = n - 127
        if 0 <= d < 128:
            oh[int(_rel_bucket_np(np.array([d]))[0]), n] = 1.0
        else:
            oh[32, n] = NEG
    return cm, blockmask, sel, oh


_CACHE = {}


def kernel(x_prompt, x_sample, state_gla, cache_win_k, cache_win_v, p_prompt, p_sample,
           norm_mix, norm_mlp, norm_ple, norm_kv, norm_final,
           w_in_a, w_a2, b_a2, gla_o_gain, w_out_a,
           w_kv, w_q_b, w_o_b, sinks, rel_bias,
           w_up, w_down, w_ple, w_ple_gate):
    f = lambda a: np.ascontiguousarray(np.asarray(a, dtype=np.float32))
    x_prompt, x_sample, state_gla = f(x_prompt), f(x_sample), f(state_gla)
    cache_win_k, cache_win_v, p_prompt, p_sample = f(cache_win_k), f(cache_win_v), f(p_prompt), f(p_sample)
    if "nc" not in _CACHE:
        _CACHE["nc"] = build()[0]
    nc = _CACHE["nc"]
    cm, blockmask, sel, oh = _consts()

    def colchunks(v):
        return np.asarray(v, np.float32).reshape(8, 128).T

    gains = np.zeros((128, 66), np.float32)
    gains[:, 0:8] = colchunks(norm_mix[0]); gains[:, 8:16] = colchunks(norm_mix[1])
    gains[:, 16:24] = colchunks(norm_mlp[0]); gains[:, 24:32] = colchunks(norm_mlp[1])
    gains[:, 32:40] = colchunks(norm_ple[0]); gains[:, 40:48] = colchunks(norm_ple[1])
    gains[:, 48:56] = colchunks(norm_kv); gains[:, 56:64] = colchunks(norm_final)
    gains[:, 64:66] = np.asarray(gla_o_gain[0], np.float32).reshape(2, 128).T
    perm = []
    for cgrp in range(2):
        for j in range(4):
            for hd in (8 * cgrp + j, 8 * cgrp + 4 + j):
                perm.extend(range(64 * hd, 64 * hd + 64))
    perm = np.array(perm)
    shared = {
        "w_in": f(w_in_a[0]), "w_a2a": f(np.concatenate([w_a2[0], b_a2[0][None]], 0)), "w_out": f(w_out_a[0]),
        "w_kv": f(w_kv), "w_q": f(np.asarray(w_q_b[0])[:, perm]), "w_o": f(np.asarray(w_o_b[0])[perm, :]),
        "w_up": f(w_up), "w_down": f(w_down), "w_ple": f(w_ple), "w_pg": f(w_ple_gate),
        "gains": gains, "sinks_b": f(np.broadcast_to(np.asarray(sinks[0], np.float32)[None], (128, 16))),
        "rb_aug": f(np.concatenate([rel_bias, np.ones((1, 16), np.float32)], 0)), "oh_aug": oh,
        "cmask": cm, "blockmask": blockmask, "sel": sel,
        "ohsel": np.ascontiguousarray(np.broadcast_to(np.eye(16, dtype=np.float32)[:, :, None], (16, 16, 128))),
    }
    in_maps = []
    for c in range(8):
        b, s = c // 4, c % 4
        end = 2048 * (s + 1)
        start = end - 8192
        xw = np.zeros((8192, 1024), np.float32)
        pw = np.zeros((2, 2176, 256), np.float32)
        lo = max(start, 0)
        xw[lo - start:] = x_prompt[b, lo:end]
        plo = max(end - 2176, 0)
        pw[:, plo - (end - 2176):] = p_prompt[:, b, plo:end]
        xs = x_sample[16 * c:16 * c + 16].reshape(128, 1024)
        ps = p_sample[:, 16 * c:16 * c + 16].reshape(2, 128, 256)
        xpre = xw[:NPREB * 128].reshape(NPREB, 128, 8, 128).transpose(0, 3, 2, 1)
        xm = np.concatenate([xw[NPREB * 128:], xs], 0)
        xmain = xm.reshape(T, 8, 128).transpose(2, 1, 0)
        pm = np.concatenate([pw, ps], 1)
        pTm = pm.reshape(2, T, 2, 128).transpose(0, 3, 2, 1)
        m = dict(shared)
        m.update({
            "xpre": f(xpre), "xmain": f(xmain), "pT": f(pTm),
            "sgla": f(state_gla[0, 16 * c:16 * c + 16]),
            "ck": f(cache_win_k[16 * c:16 * c + 16].reshape(16, 128, 256)),
            "cv": f(cache_win_v[16 * c:16 * c + 16].reshape(16, 128, 256)),
            "ckT": f(cache_win_k[16 * c:16 * c + 16].reshape(16, 128, 256).transpose(0, 2, 1)),
            "flag": np.full((128, 1), 1.0 if s > 0 else 0.0, np.float32),
        })
        in_maps.append(m)
    res = run_bass_kernel_spmd(nc, in_maps, core_ids=list(range(8)))
    R = res.results
    _CACHE["last"] = R
    y_prompt = np.zeros((2, 8192, 1024), np.float32)
    y_sample = np.zeros((128, 8, 1024), np.float32)
    st_prompt = np.zeros((1, 2, 4, 128, 256), np.float32)
    st_sample = np.zeros((1, 128, 4, 128, 256), np.float32)
    ckp = np.zeros((2, 128, 4, 64), np.float32)
    cvp = np.zeros((2, 128, 4, 64), np.float32)
    cks = np.zeros((128, 128, 4, 64), np.float32)
    cvs = np.zeros((128, 128, 4, 64), np.float32)
    for c in range(8):
        b, s = c // 4, c % 4
        yt = np.asarray(R[c]["yT"])
        ytok = yt.transpose(2, 1, 0).reshape(2176, 1024)
        y_prompt[b, 2048 * s:2048 * (s + 1)] = ytok[:2048]
        y_sample[16 * c:16 * c + 16] = ytok[2048:].reshape(16, 8, 1024)
        st_sample[0, 16 * c:16 * c + 16] = np.asarray(R[c]["st_s"])
        cks[16 * c:16 * c + 16] = np.concatenate([np.asarray(R[c]["ckc"])[:, 8:], np.asarray(R[c]["kvo"])[2].reshape(16, 8, 256)], 1).reshape(16, 128, 4, 64)
        cvs[16 * c:16 * c + 16] = np.concatenate([np.asarray(R[c]["cvc"])[:, 8:], np.asarray(R[c]["kvo"])[3].reshape(16, 8, 256)], 1).reshape(16, 128, 4, 64)
        if s == 3:
            st_prompt[0, b] = np.asarray(R[c]["st_p"])
            ckp[b] = np.asarray(R[c]["kvo"])[0].reshape(128, 4, 64)
            cvp[b] = np.asarray(R[c]["kvo"])[1].reshape(128, 4, 64)
    return (y_prompt, y_sample, st_prompt, st_sample, ckp, cvp, cks, cvs)
```
